# Optimizing a Trainium2 kernel written in Bass

```python
import jax, jax.numpy as jnp
from jax import lax
import numpy as np

D_MODEL = 1024
BATCH = 32
SEQ = 2048
DEPTH = 2

CTX_LEN = 256
GRID_W = 64
EPS = 1e-6

HEAD_DIM = 64
ATT_WIDTH = D_MODEL // 2
N_HEADS = ATT_WIDTH // HEAD_DIM
N_KV_HEADS = N_HEADS // 4
KV_WIDTH = N_KV_HEADS * HEAD_DIM
Q_BLOCK = 128
ROPE_BASE = 10000.0

CHUNK = 128
SGU_WIDTH = D_MODEL // 4
N_SGU_GROUPS = 4
SGU_GROUP_DIM = SGU_WIDTH // N_SGU_GROUPS

POOL_WINDOWS = (2, 4, 8, 16)
POOL_WIDTH = D_MODEL // 4
N_POOL_GROUPS = len(POOL_WINDOWS)
POOL_GROUP_DIM = POOL_WIDTH // N_POOL_GROUPS

MIX_WIDTH = ATT_WIDTH + SGU_WIDTH + POOL_WIDTH
Q_END = ATT_WIDTH
K_END = Q_END + KV_WIDTH
V_END = K_END + KV_WIDTH
U_END = V_END + SGU_WIDTH
SV_END = U_END + SGU_WIDTH
IN_WIDTH = SV_END + POOL_WIDTH

D_FF = 2816
CONV_WIDTH = 3

kernel_name = "hybrid_parallel_attn_sgu_pool_convffn"


def rmsnorm(x, g):
    xf = x.astype(jnp.float32)
    y = xf * lax.rsqrt(jnp.mean(xf * xf, axis=-1, keepdims=True) + EPS)
    return (y * g.astype(jnp.float32)).astype(x.dtype)


def modulate(h, shift, scale):
    return h * (1 + scale) + shift


def axial_rope_tables(n, dtype):
    rows = n // GRID_W
    row = jnp.repeat(jnp.arange(rows), GRID_W).astype(jnp.float32)
    col = jnp.tile(jnp.arange(GRID_W), rows).astype(jnp.float32)
    axis_dim = HEAD_DIM // 2
    inv = ROPE_BASE ** (-jnp.arange(0, axis_dim, 2, dtype=jnp.float32) / axis_dim)
    ang_r = row[:, None] * inv[None, :]
    ang_c = col[:, None] * inv[None, :]
    return (jnp.cos(ang_r).astype(dtype), jnp.sin(ang_r).astype(dtype),
            jnp.cos(ang_c).astype(dtype), jnp.sin(ang_c).astype(dtype))


def _rotate(xp, cos, sin):
    half = xp.shape[-1] // 2
    x1, x2 = xp[..., :half], xp[..., half:]
    c = cos[None, :, None, :]
    s = sin[None, :, None, :]
    return jnp.concatenate([x1 * c - x2 * s, x2 * c + x1 * s], axis=-1)


def apply_axial_rope(x, tabs):
    cos_r, sin_r, cos_c, sin_c = tabs
    axis_dim = HEAD_DIM // 2
    return jnp.concatenate([_rotate(x[..., :axis_dim], cos_r, sin_r),
                            _rotate(x[..., axis_dim:], cos_c, sin_c)], axis=-1)


def q_heads(p, q_gain):
    B, L, _ = p.shape
    q = p[..., :Q_END].reshape(B, L, N_HEADS, HEAD_DIM)
    return rmsnorm(q, q_gain)


def kv_heads(p, k_gain):
    B, L, _ = p.shape
    k = p[..., Q_END:K_END].reshape(B, L, N_KV_HEADS, HEAD_DIM)
    v = p[..., K_END:V_END].reshape(B, L, N_KV_HEADS, HEAD_DIM)
    return rmsnorm(k, k_gain), v


def _attend(qi, k, v):
    s = jnp.einsum('bqkgd,bnkd->bkgqn', qi, k, preferred_element_type=jnp.float32)
    pr = jax.nn.softmax(s, axis=-1).astype(v.dtype)
    return jnp.einsum('bkgqn,bnkd->bqkgd', pr, v)


def latent_attention(q, kx, vx, kc, vc):
    B, S, H, hd = q.shape
    G = H // N_KV_HEADS
    k = jnp.concatenate([kx, kc], axis=1)
    v = jnp.concatenate([vx, vc], axis=1)
    q = q * (hd ** -0.5)
    qb = q.reshape(B, S // Q_BLOCK, Q_BLOCK, N_KV_HEADS, G, hd).transpose(1, 0, 2, 3, 4, 5)
    o = lax.map(lambda qi: _attend(qi, k, v), qb)
    return o.transpose(1, 0, 2, 3, 4, 5).reshape(B, S, H * hd)


def context_attention(q, k, v):
    B, C, H, hd = q.shape
    G = H // N_KV_HEADS
    qg = (q * (hd ** -0.5)).reshape(B, C, N_KV_HEADS, G, hd)
    return _attend(qg, k, v).reshape(B, C, H * hd)


def spatial_gating(p, w_s, b_s):
    B, L, _ = p.shape
    u = p[..., V_END:U_END]
    v = p[..., U_END:SV_END].reshape(B, L // CHUNK, CHUNK, N_SGU_GROUPS, SGU_GROUP_DIM)
    mix = jnp.einsum('gpq,bnqgd->bnpgd', w_s, v) + b_s.T[None, None, :, :, None]
    return u * mix.reshape(B, L, SGU_WIDTH)


def multi_scale_pool(p, w_pool, pool_scale):
    B, L, _ = p.shape
    xp = p[..., SV_END:IN_WIDTH]
    outs = []
    for gi, w in enumerate(POOL_WINDOWS):
        xg = xp[..., gi * POOL_GROUP_DIM:(gi + 1) * POOL_GROUP_DIM].astype(jnp.float32)
        left = w // 2
        right = w - 1 - left
        cs = jnp.cumsum(jnp.pad(xg, ((0, 0), (left + 1, right), (0, 0))), axis=1)
        tot = cs[:, w:w + L] - cs[:, :L]
        t = np.arange(L)
        cnt = (np.minimum(t + right, L - 1) - np.maximum(t - left, 0) + 1).astype(np.float32)
        mean = tot / jnp.asarray(cnt)[None, :, None]
        outs.append((mean - xg).astype(p.dtype))
    y = jnp.stack(outs, axis=2)
    y = jnp.einsum('blgd,gde->blge', y, w_pool).reshape(B, L, POOL_WIDTH)
    return y * pool_scale


def conv_ffn(h, w_up, conv_w, conv_b, w_down):
    z = h @ w_up
    zp = jnp.pad(z, ((0, 0), (1, 1), (0, 0)))
    z = zp[:, :-2] * conv_w[0] + zp[:, 1:-1] * conv_w[1] + zp[:, 2:] * conv_w[2] + conv_b
    g, val = z[..., :D_FF], z[..., D_FF:]
    return (jax.nn.silu(g) * val) @ w_down


def setup_inputs(seed: int = 0) -> dict:
    key = jax.random.key(seed)
    ks = jax.random.split(key, 24)
    f32 = jnp.float32
    nrm = lambda k, shape, s: jax.random.normal(k, shape, f32) * s
    D = D_MODEL
    return {
        "x": nrm(ks[0], (BATCH, SEQ, D), 1.0),
        "c": nrm(ks[1], (BATCH, D), 1.0),
        "ctx": nrm(ks[2], (BATCH, CTX_LEN, D), 1.0),
        "c_ctx": nrm(ks[3], (D,), 1.0),
        "w_mod": nrm(ks[4], (DEPTH, D, 6 * D), 0.5 * D ** -0.5),
        "b_mod": nrm(ks[5], (DEPTH, 6 * D), 0.01),
        "norm1_g": 1.0 + nrm(ks[6], (DEPTH, D), 0.02),
        "w_in": nrm(ks[7], (DEPTH, D, IN_WIDTH), D ** -0.5),
        "q_gain": 1.0 + nrm(ks[8], (DEPTH, HEAD_DIM), 0.02),
        "k_gain": 1.0 + nrm(ks[9], (DEPTH, HEAD_DIM), 0.02),
        "w_s": nrm(ks[10], (DEPTH, N_SGU_GROUPS, CHUNK, CHUNK), CHUNK ** -0.5),
        "b_s": nrm(ks[11], (DEPTH, N_SGU_GROUPS, CHUNK), 0.01),
        "w_pool": nrm(ks[12], (DEPTH, N_POOL_GROUPS, POOL_GROUP_DIM, POOL_GROUP_DIM), POOL_GROUP_DIM ** -0.5),
        "pool_scale": 1.0 + nrm(ks[13], (DEPTH, POOL_WIDTH), 0.1),
        "w_out": nrm(ks[14], (DEPTH, MIX_WIDTH, D), MIX_WIDTH ** -0.5),
        "norm2_g": 1.0 + nrm(ks[15], (DEPTH, D), 0.02),
        "w_up": nrm(ks[16], (DEPTH, D, 2 * D_FF), D ** -0.5),
        "conv_w": nrm(ks[17], (DEPTH, CONV_WIDTH, 2 * D_FF), CONV_WIDTH ** -0.5),
        "conv_b": nrm(ks[18], (DEPTH, 2 * D_FF), 0.01),
        "w_down": nrm(ks[19], (DEPTH, D_FF, D), D_FF ** -0.5),
        "final_g": 1.0 + nrm(ks[20], (D,), 0.02),
    }


def reference(x, c, ctx, c_ctx, w_mod, b_mod, norm1_g, w_in, q_gain, k_gain, w_s, b_s,
              w_pool, pool_scale, w_out, norm2_g, w_up, conv_w, conv_b, w_down, final_g):
    tabs = axial_rope_tables(x.shape[1], x.dtype)
    silu_c = jax.nn.silu(c)
    silu_cc = jax.nn.silu(c_ctx)
    for i in range(DEPTH):
        last = i == DEPTH - 1
        mod_x = (silu_c @ w_mod[i] + b_mod[i])[:, None, :]
        mod_c = silu_cc @ w_mod[i] + b_mod[i]
        sh1, sc1, g1, sh2, sc2, g2 = jnp.split(mod_x, 6, axis=-1)
        csh1, csc1, cg1, csh2, csc2, cg2 = jnp.split(mod_c, 6, axis=-1)

        hx = modulate(rmsnorm(x, norm1_g[i]), sh1, sc1)
        hc = modulate(rmsnorm(ctx, norm1_g[i]), csh1, csc1)
        px = hx @ w_in[i]
        pc = hc @ w_in[i]

        qx = apply_axial_rope(q_heads(px, q_gain[i]), tabs)
        kx, vx = kv_heads(px, k_gain[i])
        kx = apply_axial_rope(kx, tabs)
        kc, vc = kv_heads(pc, k_gain[i])

        att_x = latent_attention(qx, kx, vx, kc, vc)
        sgu_x = spatial_gating(px, w_s[i], b_s[i])
        pool_x = multi_scale_pool(px, w_pool[i], pool_scale[i])
        mix_x = jnp.concatenate([att_x, sgu_x, pool_x], axis=-1) @ w_out[i]
        x = x + g1 * mix_x

        x = x + g2 * conv_ffn(modulate(rmsnorm(x, norm2_g[i]), sh2, sc2),
                              w_up[i], conv_w[i], conv_b[i], w_down[i])

        if not last:
            qc = q_heads(pc, q_gain[i])
            att_c = context_attention(qc, kc, vc)
            sgu_c = spatial_gating(pc, w_s[i], b_s[i])
            pool_c = multi_scale_pool(pc, w_pool[i], pool_scale[i])
            mix_c = jnp.concatenate([att_c, sgu_c, pool_c], axis=-1) @ w_out[i]
            ctx = ctx + cg1 * mix_c
            ctx = ctx + cg2 * conv_ffn(modulate(rmsnorm(ctx, norm2_g[i]), csh2, csc2),
                                       w_up[i], conv_w[i], conv_b[i], w_down[i])
    return rmsnorm(x, final_g)
```

```python
import contextlib
from collections import deque

import numpy as np
import concourse.bass as bass
import concourse.mybir as mybir
from concourse.bass_utils import run_bass_kernel_spmd

F32 = mybir.dt.float32
BF16 = mybir.dt.bfloat16
AF = mybir.ActivationFunctionType
ALU = mybir.AluOpType
AX = mybir.AxisListType

D = 1024
KC = 8
S = 2048
CL = 256
NBC = 4
DFF = 2816
NFC = 22
EPS = 1e-6
NSTAGE = 3
CG = 4
POOL_W = (2, 4, 8, 16)
N_CORES = 8


class Tok:
    __slots__ = ("sem", "val", "key")

    def __init__(self, sem, val, key):
        self.sem, self.val, self.key = sem, val, key


class Res:
    __slots__ = ("w", "r")

    def __init__(self):
        self.w = None
        self.r = {}


class Eng:
    def __init__(self, B, e, name):
        self.B, self.e, self.name = B, e, name
        self.sem = None
        self.cnt = 0
        self.seen = {}
        self.last = None

    def rotate(self):
        self.sem = self.B.new_sem()
        self.cnt = 0

    def wait(self, t):
        if t is None:
            return
        k = id(t.sem)
        if self.seen.get(k, 0) >= t.val:
            return
        self.e.wait_ge(t.sem, t.val)
        self.seen[k] = t.val

    def mark(self, ins):
        self.cnt += 1
        ins.then_inc(self.sem, 1)
        t = Tok(self.sem, self.cnt, self.name)
        self.last = t
        return t


def _prod(xs):
    r = 1
    for x in xs:
        r *= x
    return r


class Builder:
    def __init__(self, nb=NBC, dbg=False, layers=(0, 1)):
        self.nb = nb
        self.dbg = dbg
        nc = bass.Bass("TRN2", target_bir_lowering=False)
        self.nc = nc
        self.es = contextlib.ExitStack()
        self.nsem = 0
        self.sems = []
        self.dbg_outs = {}
        self.out_toks = []
        self.dbg_toks = []
        self.dbg_points = ()

    def new_sem(self):
        self.nsem += 1
        s = self.es.enter_context(self.nc.semaphore(f"s{self.nsem}"))
        self.sems.append(s)
        return s

    def op(self, E, emit, reads=(), writes=()):
        for r in reads:
            E.wait(r.w)
        for w in writes:
            E.wait(w.w)
            for t in w.r.values():
                E.wait(t)
        ins = emit()
        t = E.mark(ins)
        for r in reads:
            r.r[t.key] = t
        for w in writes:
            w.w = t
            w.r = {}
        return t

    def dma(self, out, in_, reads=(), writes=(), after_bar=False):
        Q = self.SP
        i = self.dma_i
        self.dma_i = (i + 1) % len(self.dsems)
        sem = self.dsems[i]
        prev = self.dlast[i]
        Q.wait(prev)
        if after_bar:
            for t in self.bar_toks:
                Q.wait(t)
        for r in reads:
            Q.wait(r.w)
        for w in writes:
            Q.wait(w.w)
            for t in w.r.values():
                Q.wait(t)
        ins = self.nc.sync.dma_start(out=out, in_=in_)
        ins.then_inc(sem, 16)
        v = (prev.val if prev else 0) + 16
        t = Tok(sem, v, ("d", i))
        self.dlast[i] = t
        for r in reads:
            r.r[t.key] = t
        for w in writes:
            w.w = t
            w.r = {}
        return t

    def barrier(self):
        toks = [E.last for E in self.cengs if E.last is not None]
        toks += [t for t in self.dlast if t is not None]
        for E in self.cengs:
            for t in toks:
                E.wait(t)
        self.bar_toks = [E.last for E in self.cengs if E.last is not None]

    def rotate_sems(self):
        for E in self.cengs:
            E.rotate()

    def alloc(self, free_shape, dt=F32):
        if isinstance(free_shape, int):
            free_shape = (free_shape,)
        n = _prod(free_shape)
        nb = n * (2 if dt == BF16 else 4)
        nf = (nb + 3) // 4
        nf = (nf + 15) // 16 * 16
        assert self.off + nf <= self.cap, f"SBUF overflow: {self.off}+{nf} > {self.cap}"
        ap = self.big[:, self.off:self.off + nf]
        self.off += nf
        self.peak = max(self.peak, self.off)
        if dt == BF16:
            ap = ap.bitcast(BF16)
        ap = ap[:, 0:n]
        if len(free_shape) == 2:
            ap = ap.rearrange("p (a b) -> p a b", a=free_shape[0])
        elif len(free_shape) == 3:
            ap = ap.rearrange("p (a b c) -> p a b c", a=free_shape[0], b=free_shape[1])
        return ap

    def wload(self, src, tag):
        self.loads.append((src, tag))

    def issue_loads(self):
        while self.ld_issued < len(self.loads) and self.stage_inflight < NSTAGE:
            src, tag = self.loads[self.ld_issued]
            slot = self.ld_issued % NSTAGE
            n = src.shape[-1]
            self.dma(self.stage[slot][:, 0:n], src, writes=[self.stage_res[slot]])
            self.ld_issued += 1
            self.stage_inflight += 1

    def wcast(self, tag, dst, dres, scale=None, sres=None, parts=128):
        self.issue_loads()
        src, t = self.loads[self.ld_cast]
        assert t == tag, (t, tag)
        slot = self.ld_cast % NSTAGE
        n = src.shape[-1]
        st = self.stage[slot][0:parts, 0:n]
        nc = self.nc
        if scale is None:
            self.op(self.POOL, lambda: nc.gpsimd.tensor_copy(dst, st), reads=[self.stage_res[slot]], writes=[dres])
        else:
            self.op(self.POOL, lambda: nc.gpsimd.tensor_tensor(dst, st, scale, ALU.mult),
                    reads=[self.stage_res[slot], sres], writes=[dres])
        self.ld_cast += 1
        self.stage_inflight -= 1
        self.issue_loads()

    def build(self):
        nc = self.nc
        es = self.es
        nb = self.nb
        dr = {}

        def din(name, shape):
            dr[name] = nc.dram_tensor(name, list(shape), F32, kind="ExternalInput").ap()
            return dr[name]

        x_d = din("x", [nb, S, D])
        ctx_d = din("ctx", [NBC, CL, D])
        cT_d = din("cT", [128, 40])
        wmod_d = din("w_mod", [2, D, 6 * D])
        bmod_d = din("b_mod", [2, 6 * D])
        wtm_d = din("w_in_tm", [2, D, 1024])
        wfm_d = din("w_in_fm", [2, D, 512])
        wout_d = din("w_out_p", [2, D, D])
        wup_d = din("w_up_r", [2, 44, 128, 1024])
        wdn_d = din("w_down", [2, DFF, D])
        gains_d = din("gains", [1, 256])
        wsT_d = din("wsT", [2, 128, 512])
        bs_d = din("bs", [1, 1024])
        wbd_d = din("wbd", [2, 128, 256])
        psc_d = din("pscale", [128, 4])
        convp_d = din("convp", [128, 2 * 44 * 4])
        n1g_d = din("n1g", [2, D])
        n2g_d = din("n2g", [2, D])
        fing_d = din("fing", [1, D])
        ropeC_d = din("ropeC", [128, 16 * 64])
        ropeS_d = din("ropeS", [128, 16 * 32])
        WL = S + 32
        WC = 4 * (CL + 32)
        icl_d = din("invc_lat", [128, 2 * WL])
        icc_d = din("invc_ctx", [128, 2 * WC])
        out_d = nc.dram_tensor("out", [nb, S, D], F32, kind="ExternalOutput").ap()
        mod_d = nc.dram_tensor("mod_scr", [5, 12 * D], F32).ap()
        self.dram_names = list(dr.keys())

        def dbg_out(name, shape):
            t = nc.dram_tensor(name, list(shape), F32, kind="ExternalOutput").ap()
            self.dbg_outs[name] = t
            return t

        with es:
            self.cap = 212000 // 4 // 16 * 16
            self.big = es.enter_context(nc.sbuf_tensor("big", [128, self.cap], F32))
            self.off = 0
            self.peak = 0
            banks = [es.enter_context(nc.psum_tensor(f"bk{i}", [128, 512], F32)) for i in range(8)]
            pres = [Res() for _ in range(8)]
            self.PE = Eng(self, nc.tensor, "pe")
            self.ACT = Eng(self, nc.scalar, "act")
            self.DVE = Eng(self, nc.vector, "dve")
            self.POOL = Eng(self, nc.gpsimd, "pool")
            self.SP = Eng(self, nc.sync, "sp")
            self.cengs = [self.PE, self.ACT, self.DVE, self.POOL]
            self.rotate_sems()
            self.dsems = [self.new_sem() for _ in range(24)]
            self.dlast = [None] * 24
            self.dma_i = 0
            self.bar_toks = []
            self.loads = []
            self.ld_issued = 0
            self.ld_cast = 0
            self.stage_inflight = 0
            PE, ACT, DVE, POOL = self.PE, self.ACT, self.DVE, self.POOL
            op = self.op
            dma = self.dma
            alloc = self.alloc

            def bank(i):
                return banks[i][:, :]

            def bankb(i):
                return banks[i][:, :].bitcast(BF16)

            X = alloc((16, D))
            xres = [Res() for _ in range(16)]
            modA = alloc(D)
            modB = alloc(D)
            modA_r, modB_r = Res(), Res()
            self.stage = [alloc(1024) for _ in range(NSTAGE)]
            self.stage_res = [Res() for _ in range(NSTAGE)]
            G = alloc((2, 128))
            wsT = alloc((2, 512), BF16)
            bsb = alloc(1024, BF16)
            wbd = alloc((2, 256), BF16)
            psc = alloc(4)
            convp = alloc(2 * 44 * 4)
            ident = alloc(128, BF16)
            ones_f = alloc(128)
            ones_b = alloc(128, BF16)
            epsb = alloc(1)
            ssq = alloc(16)
            rstd = alloc(16)
            kTc = alloc((2, NBC * CL), BF16)
            VAc = alloc((2, 2 * NBC, 65), BF16)
            VBc = alloc((2, 2 * NBC, 128), BF16)
            const_r = Res()
            ssq_r, rstd_r = Res(), Res()
            kvc_r = [[Res() for _ in range(2 * NBC)] for _ in range(2)]
            persist_end = self.off

            m0 = self.off
            identf = alloc(128)
            idr = Res()
            op(POOL, lambda: nc.gpsimd.memset(identf, 0.0), writes=[idr])
            op(POOL, lambda: nc.gpsimd.affine_select(out=identf, in_=identf, pattern=[[-1, 128]],
                                                     compare_op=ALU.not_equal, fill=1.0, base=0,
                                                     channel_multiplier=1), reads=[idr], writes=[idr])
            op(POOL, lambda: nc.gpsimd.tensor_copy(ident, identf), reads=[idr], writes=[const_r])
            op(POOL, lambda: nc.gpsimd.memset(ones_f, 1.0), writes=[const_r])
            op(POOL, lambda: nc.gpsimd.memset(ones_b, 1.0), writes=[const_r])
            op(POOL, lambda: nc.gpsimd.memset(epsb, EPS), writes=[const_r])
            op(POOL, lambda: nc.gpsimd.memset(VAc[:, :, :, 64:65], 1.0), writes=[const_r])
            op(POOL, lambda: nc.gpsimd.memset(VBc[:, :, :, 0:64], 0.0), writes=[const_r])
            op(POOL, lambda: nc.gpsimd.memset(VBc[:, :, :, 0:1], 1.0), reads=[const_r], writes=[const_r])
            small_r = Res()
            tl = []
            tl.append(dma(G.rearrange("p a b -> p (a b)"),
                          gains_d[0:1, :].partition_broadcast(128), writes=[small_r]))
            tl.append(dma(psc, psc_d[:, :], writes=[Res()]))
            tl.append(dma(convp, convp_d[:, :], writes=[Res()]))
            for E in self.cengs:
                for t in tl:
                    E.wait(t)
            self.wload(wsT_d[0], "wsT0")
            self.wload(wsT_d[1], "wsT1")
            self.wload(wbd_d[0], "wbd0")
            self.wload(wbd_d[1], "wbd1")
            self.wcast("wsT0", wsT[:, 0, :], const_r)
            self.wcast("wsT1", wsT[:, 1, :], const_r)
            self.wcast("wbd0", wbd[:, 0, :], const_r)
            self.wcast("wbd1", wbd[:, 1, :], const_r)
            bs_st = alloc(1024)
            bsr = Res()
            dma(bs_st[0:1, :], bs_d[0:1, :], writes=[bsr])
            op(POOL, lambda: nc.gpsimd.tensor_copy(bsb[0:1, :], bs_st[0:1, :]), reads=[bsr], writes=[const_r])

            sc = alloc((8, 5))
            scr = Res()
            dma(sc.rearrange("p a b -> p (a b)"), cT_d[:, :], writes=[scr])
            op(ACT, lambda: nc.scalar.activation(out=sc, in_=sc, func=AF.Silu), reads=[scr], writes=[scr])
            wm = [alloc((8, 512)) for _ in range(2)]
            wm_r = [Res(), Res()]
            bm = [alloc(512) for _ in range(2)]
            bm_r = [Res(), Res()]
            mrow = [alloc(512) for _ in range(2)]
            mrow_r = [Res(), Res()]
            mod_res = [[Res() for _ in range(12)] for _ in range(2)]
            it = 0
            for l in range(2):
                wsrc = wmod_d[l].rearrange("(kc p) n -> p kc n", p=128)
                for n in range(12):
                    k = it % 2
                    it += 1
                    dma(wm[k], wsrc[:, :, n * 512:(n + 1) * 512], writes=[wm_r[k]])
                    dma(bm[k][0:5, :], bmod_d[l:l + 1, n * 512:(n + 1) * 512].partition_broadcast(5), writes=[bm_r[k]])

                    def mm(k=k):
                        ins = None
                        for kc in range(KC):
                            ins = nc.tensor.matmul(bank(k)[0:5, :], sc[:, kc, :], wm[k][:, kc, :],
                                                   start=(kc == 0), stop=(kc == KC - 1))
                        return ins
                    op(PE, mm, reads=[scr, wm_r[k]], writes=[pres[k]])
                    op(DVE, lambda k=k: nc.vector.tensor_tensor(mrow[k][0:5, :], bank(k)[0:5, :], bm[k][0:5, :], ALU.add),
                       reads=[pres[k], bm_r[k]], writes=[mrow_r[k]])
                    dma(mod_d[0:5, l * 6 * D + n * 512: l * 6 * D + (n + 1) * 512], mrow[k][0:5, :],
                        reads=[mrow_r[k]], writes=[mod_res[l][n]])
            self.barrier()
            self.off = persist_end

            def load_mod(dst, dres, row, l, v):
                src = mod_d[row:row + 1, l * 6 * D + v * D: l * 6 * D + (v + 1) * D].partition_broadcast(128)
                return dma(dst, src, reads=[mod_res[l][2 * v], mod_res[l][2 * v + 1]], writes=[dres])

            def load_row(dst, dres, src_row):
                return dma(dst, src_row.partition_broadcast(128), writes=[dres], after_bar=True)

            def enqueue_layer(l, kv_only):
                if kv_only:
                    for kc in range(KC):
                        self.wload(wtm_d[l, kc * 128:(kc + 1) * 128, 512:768], f"kv{l}_{kc}")
                    return
                for kc in range(KC):
                    self.wload(wtm_d[l, kc * 128:(kc + 1) * 128, :], f"tm{l}_{kc}")
                for kc in range(KC):
                    self.wload(wfm_d[l, kc * 128:(kc + 1) * 128, :], f"fm{l}_{kc}")
                for kc in range(KC):
                    self.wload(wout_d[l, kc * 128:(kc + 1) * 128, :], f"wo{l}_{kc}")
                for j in range(NFC):
                    self.wload(wup_d[l, 2 * j, :, :], f"ug{l}_{j}")
                    self.wload(wup_d[l, 2 * j + 1, :, :], f"uv{l}_{j}")
                    self.wload(wdn_d[l, j * 128:(j + 1) * 128, :], f"dn{l}_{j}")

            def norm_stats(NT, junk, junk_r):
                for i in range(NT):
                    op(ACT, lambda i=i: nc.scalar.activation(out=junk, in_=X[:, i, :], func=AF.Square,
                                                              scale=1.0 / 32.0, accum_out=ssq[:, i:i + 1]),
                       reads=[xres[i]], writes=[junk_r, ssq_r])
                op(ACT, lambda: nc.scalar.activation(out=ssq[:, 0:NT], in_=ssq[:, 0:NT], func=AF.Sqrt,
                                                     bias=epsb[:, 0:1], scale=1.0),
                   reads=[ssq_r, const_r], writes=[ssq_r])
                op(DVE, lambda: nc.vector.reciprocal(rstd[:, 0:NT], ssq[:, 0:NT]), reads=[ssq_r], writes=[rstd_r])

            def make_gm(row, l, v_scale, v_shift, ng_d, nrm, nrm_r):
                load_mod(modA, modA_r, row, l, v_scale)
                load_row(nrm, nrm_r, ng_d[l:l + 1, :])
                load_mod(modB, modB_r, row, l, v_shift)
                op(DVE, lambda: nc.vector.scalar_tensor_tensor(modA, modA, 1.0, nrm, ALU.add, ALU.mult),
                   reads=[nrm_r], writes=[modA_r])

            def norm_tile(i, nrm, nrm_r, hb, hb_r):
                op(DVE, lambda: nc.vector.scalar_tensor_tensor(nrm, X[:, i, :], rstd[:, i:i + 1], modA, ALU.mult, ALU.mult),
                   reads=[xres[i], rstd_r, modA_r], writes=[nrm_r])
                op(DVE, lambda: nc.vector.tensor_tensor(hb, nrm, modB, ALU.add), reads=[nrm_r, modB_r], writes=[hb_r])

            def transpose8(src, src_r, bk, dst, dst_r, n=8):
                def tr():
                    ins = None
                    for kc in range(n):
                        ins = nc.tensor.transpose(bankb(bk)[:, kc * 128:(kc + 1) * 128], src[:, kc * 128:(kc + 1) * 128], ident)
                    return ins
                op(PE, tr, reads=[src_r, const_r], writes=[pres[bk]])

            def run_layer(kind, row, l, NT, seqs, L, kv_only, b):
                is_ctx = kind == "ctx"
                T = NT * 128
                NSEQ = len(seqs)
                W = NSEQ * (L + 32)
                if l == 0:
                    self.rotate_sems()
                base = self.off

                def tokpos(i):
                    s = (i * 128) // L
                    return s * (L + 32) + 16 + (i * 128 - s * L)

                if not kv_only:
                    qT = alloc((4, T), BF16)
                    qT_r = Res()
                    sguT = alloc((2, T), BF16)
                    sguT_r = Res()
                    xpT = alloc((2, W))
                    xpT_r = Res()
                if is_ctx:
                    kT = kTc[:, l, :]
                    VA = VAc[:, l, :, :]
                    VB = VBc[:, l, :, :]
                    kv_r = kvc_r[l]
                else:
                    kT = alloc(T, BF16)
                    VA = alloc((NT, 65), BF16)
                    VB = alloc((NT, 128), BF16)
                    kv_r = [Res() for _ in range(NT)]
                    vinit = Res()
                    op(POOL, lambda: nc.gpsimd.memset(VA[:, :, 64:65], 1.0), writes=[vinit])
                    op(POOL, lambda: nc.gpsimd.memset(VB[:, :, 0:64], 0.0), writes=[vinit])
                    op(POOL, lambda: nc.gpsimd.memset(VB[:, :, 0:1], 1.0), reads=[vinit], writes=[vinit])
                m1 = self.off
                nrm = alloc(D)
                nrm_r = Res()
                hb = alloc(D, BF16)
                hb_r = Res()
                hT = alloc((8, 128), BF16)
                hT_r = Res()
                ncol = 256 if kv_only else 1024
                w_tm = alloc((8, ncol), BF16)
                wtm_r = [Res() for _ in range(KC)]
                qkv = alloc(ncol)
                qkv_r = Res()
                t1 = alloc(640)
                t2 = alloc(640)
                t1_r, t2_r = Res(), Res()
                qkb = alloc(640, BF16)
                qkb_r = Res()
                s10 = alloc(16)
                s10_r = Res()
                if not kv_only:
                    w_fm = alloc((8, 512), BF16)
                    wfm_r = [Res() for _ in range(KC)]
                    u_sb = alloc((2, 128))
                    u_r = Res()
                    svb = alloc(256, BF16)
                    svb_r = Res()
                    t3 = alloc(640)
                    t3_r = Res()
                    if not is_ctx:
                        rC = alloc((16, 64))
                        rS = alloc((16, 32))
                        rope_r = Res()
                        dma(rC.rearrange("p a b -> p (a b)"), ropeC_d[:, :], writes=[rope_r], after_bar=True)
                        dma(rS.rearrange("p a b -> p (a b)"), ropeS_d[:, :], writes=[rope_r], after_bar=True)
                        op(POOL, lambda: nc.gpsimd.memset(xpT, 0.0), writes=[xpT_r])
                    else:
                        op(POOL, lambda: nc.gpsimd.memset(xpT, 0.0), writes=[xpT_r])

                make_gm(row, l, 1, 0, n1g_d, nrm, nrm_r)
                if kv_only:
                    for kc in range(KC):
                        self.wcast(f"kv{l}_{kc}", w_tm[:, kc, :], wtm_r[kc])
                else:
                    for kc in range(KC):
                        self.wcast(f"tm{l}_{kc}", w_tm[:, kc, :], wtm_r[kc])
                    for kc in range(KC):
                        self.wcast(f"fm{l}_{kc}", w_fm[:, kc, :], wfm_r[kc])
                norm_stats(NT, hb, hb_r)

                kcol0 = 0 if kv_only else 512
                nqk = 2 if kv_only else 10
                qk0 = 0 if kv_only else 0

                def stage_a(i):
                    norm_tile(i, nrm, nrm_r, hb, hb_r)
                    transpose8(hb, hb_r, 0, None, None)
                    op(ACT, lambda: nc.scalar.copy(hT.rearrange("p a b -> p (a b)"), bankb(0)), reads=[pres[0]], writes=[hT_r])
                    if kv_only:
                        def mm():
                            ins = None
                            for kc in range(KC):
                                ins = nc.tensor.matmul(bank(1)[:, 0:256], hT[:, kc, :], w_tm[:, kc, :],
                                                       start=(kc == 0), stop=(kc == KC - 1))
                            return ins
                        op(PE, mm, reads=[hT_r] + wtm_r, writes=[pres[1]])
                        op(ACT, lambda: nc.scalar.copy(qkv, bank(1)[:, 0:256]), reads=[pres[1]], writes=[qkv_r])
                        return
                    for n in range(2):
                        def mm(n=n):
                            ins = None
                            for kc in range(KC):
                                ins = nc.tensor.matmul(bank(1 + n), hT[:, kc, :], w_tm[:, kc, n * 512:(n + 1) * 512],
                                                       start=(kc == 0), stop=(kc == KC - 1))
                            return ins
                        op(PE, mm, reads=[hT_r] + wtm_r, writes=[pres[1 + n]])

                    def mmf():
                        ins = None
                        for cc in range(4):
                            for kc in range(KC):
                                ins = nc.tensor.matmul(bank(3)[:, cc * 128:(cc + 1) * 128], w_fm[:, kc, cc * 128:(cc + 1) * 128],
                                                       hT[:, kc, :], start=(kc == 0), stop=(kc == KC - 1))
                        return ins
                    op(PE, mmf, reads=[hT_r] + wfm_r, writes=[pres[3]])
                    op(ACT, lambda: nc.scalar.copy(qkv[:, 0:512], bank(1)), reads=[pres[1]], writes=[qkv_r])
                    op(ACT, lambda: nc.scalar.copy(qkv[:, 512:1024], bank(2)), reads=[pres[2], qkv_r], writes=[qkv_r])
                    op(ACT, lambda: nc.scalar.copy(u_sb.rearrange("p a b -> p (a b)"), bank(3)[:, 0:256]),
                       reads=[pres[3]], writes=[u_r])
                    p0 = tokpos(i)
                    op(ACT, lambda: nc.scalar.copy(xpT[:, :, p0:p0 + 128],
                                                   bank(3)[:, 256:512].rearrange("p (a b) -> p a b", a=2)),
                       reads=[pres[3], xpT_r], writes=[xpT_r])

                def stage_b(i):
                    nh = nqk
                    w = nh * 64
                    QK = qkv[:, 0:w]
                    QK3 = QK.rearrange("p (h d) -> p h d", h=nh)
                    T1 = t1[:, 0:w]
                    T2 = t2[:, 0:w]
                    T23 = T2.rearrange("p (h d) -> p h d", h=nh)
                    op(DVE, lambda: nc.vector.scalar_tensor_tensor(T1, QK, 1.0 / 64.0, QK, ALU.mult, ALU.mult),
                       reads=[qkv_r], writes=[t1_r])
                    op(DVE, lambda: nc.vector.tensor_reduce(out=s10[:, 0:nh], in_=T1.rearrange("p (h d) -> p h d", h=nh),
                                                            axis=AX.X, op=ALU.add), reads=[t1_r], writes=[s10_r])
                    op(ACT, lambda: nc.scalar.activation(out=s10[:, 0:nh], in_=s10[:, 0:nh], func=AF.Sqrt,
                                                         bias=epsb[:, 0:1], scale=1.0), reads=[s10_r, const_r], writes=[s10_r])
                    op(DVE, lambda: nc.vector.reciprocal(s10[:, 0:nh], s10[:, 0:nh]), reads=[s10_r], writes=[s10_r])
                    op(DVE, lambda: nc.vector.tensor_tensor(T23, QK3, s10[:, 0:nh].unsqueeze(2).to_broadcast([128, nh, 64]), ALU.mult),
                       reads=[qkv_r, s10_r], writes=[t2_r])
                    rope = (not is_ctx) and (not kv_only)
                    gq = G[:, l, 0:64].unsqueeze(1)
                    gk = G[:, l, 64:128].unsqueeze(1)
                    if rope:
                        op(DVE, lambda: nc.vector.tensor_tensor(T23[:, 0:8, :], T23[:, 0:8, :], gq.to_broadcast([128, 8, 64]), ALU.mult),
                           reads=[t2_r, small_r], writes=[t2_r])
                        op(DVE, lambda: nc.vector.tensor_tensor(T23[:, 8:10, :], T23[:, 8:10, :], gk.to_broadcast([128, 2, 64]), ALU.mult),
                           reads=[t2_r, small_r], writes=[t2_r])
                        cosb = rC[:, i, :].unsqueeze(1).to_broadcast([128, 10, 64])
                        op(DVE, lambda: nc.vector.tensor_tensor(t1.rearrange("p (h d) -> p h d", h=10), T23, cosb, ALU.mult),
                           reads=[t2_r, rope_r], writes=[t1_r])
                        V2 = t2.rearrange("p (h a x f) -> p h a x f", h=10, a=2, x=2)
                        V3 = t3.rearrange("p (h a x f) -> p h a x f", h=10, a=2, x=2)
                        V1 = t1.rearrange("p (h a x f) -> p h a x f", h=10, a=2, x=2)
                        VO = qkb.rearrange("p (h a x f) -> p h a x f", h=10, a=2, x=2)
                        sinb = rS[:, i, :].rearrange("p (a f) -> p a f", a=2).unsqueeze(1).to_broadcast([128, 10, 2, 16])
                        op(DVE, lambda: nc.vector.tensor_tensor(V3[:, :, :, 0, :], V2[:, :, :, 1, :], sinb, ALU.mult),
                           reads=[t2_r, rope_r], writes=[t3_r])
                        op(DVE, lambda: nc.vector.tensor_tensor(V3[:, :, :, 1, :], V2[:, :, :, 0, :], sinb, ALU.mult),
                           reads=[t2_r, rope_r, t3_r], writes=[t3_r])
                        op(DVE, lambda: nc.vector.tensor_tensor(VO[:, :, :, 0, :], V1[:, :, :, 0, :], V3[:, :, :, 0, :], ALU.subtract),
                           reads=[t1_r, t3_r], writes=[qkb_r])
                        op(DVE, lambda: nc.vector.tensor_tensor(VO[:, :, :, 1, :], V1[:, :, :, 1, :], V3[:, :, :, 1, :], ALU.add),
                           reads=[t1_r, t3_r, qkb_r], writes=[qkb_r])
                    else:
                        QB3 = qkb[:, 0:w].rearrange("p (h d) -> p h d", h=nh)
                        if kv_only:
                            op(DVE, lambda: nc.vector.tensor_tensor(QB3, T23, gk.to_broadcast([128, 2, 64]), ALU.mult),
                               reads=[t2_r, small_r], writes=[qkb_r])
                        else:
                            op(DVE, lambda: nc.vector.tensor_tensor(QB3[:, 0:8, :], T23[:, 0:8, :], gq.to_broadcast([128, 8, 64]), ALU.mult),
                               reads=[t2_r, small_r], writes=[qkb_r])
                            op(DVE, lambda: nc.vector.tensor_tensor(QB3[:, 8:10, :], T23[:, 8:10, :], gk.to_broadcast([128, 2, 64]), ALU.mult),
                               reads=[t2_r, small_r, qkb_r], writes=[qkb_r])
                    nsl = 1 if kv_only else 5

                    def tr():
                        ins = None
                        for j in range(nsl):
                            ins = nc.tensor.transpose(bankb(4)[:, j * 128:(j + 1) * 128], qkb[:, j * 128:(j + 1) * 128], ident)
                        return ins
                    op(PE, tr, reads=[qkb_r, const_r], writes=[pres[4]])
                    t0 = i * 128
                    if kv_only:
                        op(ACT, lambda: nc.scalar.copy(kT[:, t0:t0 + 128], bankb(4)[:, 0:128]), reads=[pres[4]], writes=[kv_r[i]])
                        vcol = 128
                    else:
                        op(ACT, lambda: nc.scalar.copy(qT[:, :, t0:t0 + 128], bankb(4)[:, 0:512].rearrange("p (a b) -> p a b", a=4)),
                           reads=[pres[4], qT_r], writes=[qT_r])
                        op(ACT, lambda: nc.scalar.copy(kT[:, t0:t0 + 128], bankb(4)[:, 512:640]), reads=[pres[4]], writes=[kv_r[i]])
                        vcol = 640
                    op(ACT, lambda: nc.scalar.copy(VA[:, i, 0:64], qkv[:, vcol:vcol + 64]), reads=[qkv_r, kv_r[i]], writes=[kv_r[i]])
                    op(ACT, lambda: nc.scalar.copy(VB[:, i, 64:128], qkv[:, vcol + 64:vcol + 128]), reads=[qkv_r, kv_r[i]], writes=[kv_r[i]])
                    if kv_only:
                        return
                    op(ACT, lambda: nc.scalar.copy(svb, qkv[:, 768:1024]), reads=[qkv_r], writes=[svb_r])

                    def sg():
                        ins = None
                        for c in range(2):
                            for r in range(2):
                                g = 2 * c + r
                                o = bank(5)[:, g * 128:(g + 1) * 128]
                                nc.tensor.matmul(o, svb[:, c * 128:(c + 1) * 128], wsT[:, l, g * 128:(g + 1) * 128], start=True, stop=False)
                                ins = nc.tensor.matmul(o, ones_b[0:1, 0:128], bsb[0:1, l * 512 + g * 128: l * 512 + (g + 1) * 128],
                                                       start=False, stop=True)
                        return ins
                    op(PE, sg, reads=[svb_r, const_r], writes=[pres[5]])
                    for c in range(2):
                        for r in range(2):
                            g = 2 * c + r
                            rows = slice(r * 64, (r + 1) * 64)
                            op(DVE, lambda c=c, g=g, rows=rows: nc.vector.tensor_tensor(
                                sguT[rows, c, t0:t0 + 128], bank(5)[rows, g * 128:(g + 1) * 128], u_sb[rows, c, :], ALU.mult),
                               reads=[pres[5], u_r, sguT_r], writes=[sguT_r])

                stage_a(0)
                for i in range(NT):
                    stage_b(i)
                    if i + 1 < NT:
                        stage_a(i + 1)
                if kv_only:
                    self.barrier()
                    self.off = base
                    return

                self.barrier()
                self.off = m1
                poolT = alloc((2, T), BF16)
                poolT_r = Res()
                m1 = self.off
                ta = alloc(W)
                tb = alloc(W)
                invc = alloc(W)
                yb = alloc(W, BF16)
                ta_r, tb_r, invc_r, yb_r = Res(), Res(), Res(), Res()
                ic_d = icc_d if is_ctx else icl_d
                for c in range(2):
                    dma(invc, ic_d[:, c * W:(c + 1) * W], writes=[invc_r], after_bar=True)
                    xc = xpT[:, c, :]
                    for r in range(2):
                        rows = slice(r * 64, (r + 1) * 64)
                        wdw = POOL_W[2 * c + r]
                        m = wdw.bit_length() - 1
                        bufs = [(ta, ta_r), (tb, tb_r)]
                        order = []
                        cur = 0 if (m % 2 == 1) else 1
                        for _ in range(m):
                            order.append(bufs[cur])
                            cur ^= 1
                        src, src_r = xc, xpT_r
                        for k in range(1, m):
                            dst, dst_r = order[k - 1]
                            h = 1 << (k - 1)
                            n = W - (1 << k) + 1
                            op(DVE, lambda dst=dst, src=src, h=h, n=n, rows=rows: nc.vector.tensor_tensor(
                                dst[rows, 0:n], src[rows, 0:n], src[rows, h:h + n], ALU.add),
                               reads=[src_r, dst_r], writes=[dst_r])
                            src, src_r = dst, dst_r
                        dst, dst_r = order[m - 1]
                        assert dst is ta
                        h = wdw // 2
                        n = W - wdw + 1
                        op(DVE, lambda dst=dst, src=src, h=h, n=n, rows=rows: nc.vector.tensor_tensor(
                            dst[rows, h:h + n], src[rows, 0:n], src[rows, h:h + n], ALU.add),
                           reads=[src_r, dst_r], writes=[dst_r])
                    op(DVE, lambda: nc.vector.tensor_tensor(tb[:, 16:W - 16], ta[:, 16:W - 16], invc[:, 16:W - 16], ALU.mult),
                       reads=[ta_r, invc_r, tb_r], writes=[tb_r])
                    op(DVE, lambda xc=xc: nc.vector.tensor_tensor(yb[:, 16:W - 16], tb[:, 16:W - 16], xc[:, 16:W - 16], ALU.subtract),
                       reads=[tb_r, xpT_r], writes=[yb_r])
                    blk = min(512, L)
                    bi = 0
                    for s in range(NSEQ):
                        for q0 in range(0, L, blk):
                            p0 = s * (L + 32) + 16 + q0
                            tk = s * L + q0
                            bk = 6 + (bi % 2)
                            bi += 1
                            op(PE, lambda bk=bk, p0=p0, c=c: nc.tensor.matmul(bank(bk)[:, 0:blk], wbd[:, l, c * 128:(c + 1) * 128],
                                                                               yb[:, p0:p0 + blk], start=True, stop=True),
                               reads=[yb_r, const_r], writes=[pres[bk]])
                            op(ACT, lambda bk=bk, tk=tk, c=c: nc.scalar.activation(out=poolT[:, c, tk:tk + blk], in_=bank(bk)[:, 0:blk],
                                                                                   func=AF.Copy, scale=psc[:, 2 * l + c:2 * l + c + 1]),
                               reads=[pres[bk], poolT_r], writes=[poolT_r])

                self.barrier()
                self.off = m1
                attT = alloc((4, T), BF16)
                attT_r = Res()
                pT = [alloc(512, BF16) for _ in range(4)]
                pT_r = [Res() for _ in range(4)]
                rec = [alloc(512) for _ in range(2)]
                rec_r = [Res(), Res()]
                bcs = [alloc(512) for _ in range(2)]
                bcs_r = [Res(), Res()]
                QB = min(512, L)
                blkno = 0
                pending = None
                for (st, ntl) in seqs:
                    keys = []
                    for n in range(ntl):
                        ti = st + n
                        keys.append((kT, ti * 128, VA[:, ti, :], VB[:, ti, :], kv_r[ti]))
                    if not is_ctx:
                        for n in range(2):
                            ti = b * 2 + n
                            keys.append((kTc[:, l, :], ti * 128, VAc[:, l, ti, :], VBc[:, l, ti, :], kvc_r[l][ti]))
                    NK = len(keys)
                    for q0 in range(st * 128, (st + ntl) * 128, QB):
                        for j in range(4):
                            for hh in range(2):
                                rows = slice(hh * 64, (hh + 1) * 64)
                                ub = 4 + (blkno % 2)
                                rb = blkno % 2
                                blkno += 1
                                M = 65 if hh == 0 else 128

                                def pv(n, ub=ub, hh=hh, M=M):
                                    kt, k0, va, vb, kr = keys[n]
                                    vv = va if hh == 0 else vb
                                    op(PE, lambda: nc.tensor.matmul(bank(ub)[0:M, 0:QB], vv, pT[n % 4][:, 0:QB],
                                                                    start=(n == 0), stop=(n == NK - 1)),
                                       reads=[kr, pT_r[n % 4]], writes=[pres[ub]])
                                for n in range(NK):
                                    kt, k0, va, vb, kr = keys[n]
                                    sb = n % 4
                                    op(PE, lambda kt=kt, k0=k0, sb=sb: nc.tensor.matmul(bank(sb)[:, 0:QB], kt[rows, k0:k0 + 128],
                                                                                      qT[rows, j, q0:q0 + QB], start=True, stop=True),
                                       reads=[kr, qT_r], writes=[pres[sb]])
                                    op(ACT, lambda sb=sb: nc.scalar.activation(out=pT[sb][:, 0:QB], in_=bank(sb)[:, 0:QB],
                                                                               func=AF.Exp, scale=0.125),
                                       reads=[pres[sb]], writes=[pT_r[sb]])
                                    if n == 1 and pending is not None:
                                        pending()
                                        pending = None
                                    if n >= 2:
                                        pv(n - 2)
                                for n in range(max(0, NK - 2), NK):
                                    pv(n)

                                def fin(ub=ub, rb=rb, hh=hh, rows=rows, j=j, q0=q0):
                                    sp = 64 if hh == 0 else 0
                                    op(DVE, lambda: nc.vector.reciprocal(rec[rb][sp:sp + 1, 0:QB], bank(ub)[sp:sp + 1, 0:QB]),
                                       reads=[pres[ub]], writes=[rec_r[rb]])
                                    mo = 64 if hh == 0 else 128
                                    op(PE, lambda: nc.tensor.matmul(bank(6)[0:mo, 0:QB], ones_f[sp:sp + 1, 0:mo], rec[rb][sp:sp + 1, 0:QB],
                                                                    start=True, stop=True),
                                       reads=[rec_r[rb], const_r], writes=[pres[6]])
                                    op(ACT, lambda: nc.scalar.copy(bcs[rb][rows, 0:QB], bank(6)[rows, 0:QB]),
                                       reads=[pres[6]], writes=[bcs_r[rb]])
                                    op(DVE, lambda: nc.vector.tensor_tensor(attT[rows, j, q0:q0 + QB], bank(ub)[rows, 0:QB],
                                                                            bcs[rb][rows, 0:QB], ALU.mult),
                                       reads=[pres[ub], bcs_r[rb], attT_r], writes=[attT_r])
                                if NK < 3:
                                    fin()
                                else:
                                    pending = fin
                if pending is not None:
                    pending()
                    pending = None

                m3 = self.off
                w_o = alloc((8, D), BF16)
                wo_r = [Res() for _ in range(KC)]
                load_mod(modA, modA_r, row, l, 2)
                for kc in range(KC):
                    self.wcast(f"wo{l}_{kc}", w_o[:, kc, :], wo_r[kc], scale=modA, sres=modA_r)
                feats = [(attT, attT_r, j) for j in range(4)] + [(sguT, sguT_r, c) for c in range(2)] + \
                        [(poolT, poolT_r, c) for c in range(2)]
                for i in range(NT):
                    t0 = i * 128
                    for n in range(2):
                        bk = 2 * (i % 2) + n

                        def mm(n=n, bk=bk):
                            ins = None
                            for kc, (ft, fr, idx) in enumerate(feats):
                                ins = nc.tensor.matmul(bank(bk), ft[:, idx, t0:t0 + 128], w_o[:, kc, n * 512:(n + 1) * 512],
                                                       start=(kc == 0), stop=(kc == KC - 1))
                            return ins
                        op(PE, mm, reads=[attT_r, sguT_r, poolT_r] + wo_r, writes=[pres[bk]])
                        op(DVE, lambda n=n, bk=bk: nc.vector.tensor_tensor(X[:, i, n * 512:(n + 1) * 512], bank(bk),
                                                                           X[:, i, n * 512:(n + 1) * 512], ALU.add),
                           reads=[pres[bk], xres[i]], writes=[xres[i]])
                if self.dbg and (kind, b, l) in self.dbg_points:
                    self.dump(f"xmix_{kind}{b}_{l}", X, xres, NT)

                self.barrier()
                self.off = base
                h2T = alloc((8, T), BF16)
                h2_r = [Res() for _ in range(NT)]
                nrm = alloc(D)
                nrm_r = Res()
                hb = alloc(D, BF16)
                hb_r = Res()
                make_gm(row, l, 4, 3, n2g_d, nrm, nrm_r)
                norm_stats(NT, hb, hb_r)
                for i in range(NT):
                    t0 = i * 128
                    bk = i % 2
                    norm_tile(i, nrm, nrm_r, hb, hb_r)
                    transpose8(hb, hb_r, bk, None, None)
                    op(ACT, lambda bk=bk, t0=t0: nc.scalar.copy(h2T[:, :, t0:t0 + 128], bankb(bk).rearrange("p (a b) -> p a b", a=8)),
                       reads=[pres[bk]], writes=[h2_r[i]])

                load_mod(modA, modA_r, row, l, 5)
                upg = [[alloc((8, 128), BF16) for _ in range(CG)] for _ in range(2)]
                upv = [[alloc((8, 128), BF16) for _ in range(CG)] for _ in range(2)]
                dn = [[alloc(D, BF16) for _ in range(CG)] for _ in range(2)]
                ring_r = [[[Res() for _ in range(3)] for _ in range(CG)] for _ in range(2)]
                aT = [alloc((CG, 256), BF16) for _ in range(2)]
                aT_r = [Res(), Res()]
                tg = [alloc(256) for _ in range(2)]
                tv = [alloc(256) for _ in range(2)]
                tg_r = [Res(), Res()]
                tv_r = [Res(), Res()]
                groups = [list(range(g0, min(g0 + CG, NFC))) for g0 in range(0, NFC, CG)]

                def cast_group(gi):
                    sl = gi % 2
                    for ci, jf in enumerate(groups[gi]):
                        self.wcast(f"ug{l}_{jf}", upg[sl][ci].rearrange("p a b -> p (a b)"), ring_r[sl][ci][0])
                        self.wcast(f"uv{l}_{jf}", upv[sl][ci].rearrange("p a b -> p (a b)"), ring_r[sl][ci][1])
                        self.wcast(f"dn{l}_{jf}", dn[sl][ci], ring_r[sl][ci][2], scale=modA, sres=modA_r)

                blocks = []
                for (st, ntl) in seqs:
                    nblk = ntl // 2
                    for bb in range(nblk):
                        blocks.append((st * 128 + bb * 256, bb == 0, bb == nblk - 1))
                cast_group(0)
                zc = 0
                for gi, chunks in enumerate(groups):
                    sl = gi % 2
                    if gi + 1 < len(groups):
                        cast_group(gi + 1)
                    for bidx, (t0, first, last) in enumerate(blocks):
                        ab = bidx % 2
                        lo_c = 1 if first else 0
                        hi_c = 257 if last else 258
                        lo = 1 if first else 0
                        hi = 255 if last else 256
                        tiles = [t0 // 128, t0 // 128 + 1]
                        for ci, jf in enumerate(chunks):
                            zz = zc % 2
                            zc += 1
                            for part in range(2):
                                wsb = (upg if part == 0 else upv)[sl][ci]
                                zb = zz * 2 + part
                                tt, tt_r = (tg[zz], tg_r[zz]) if part == 0 else (tv[zz], tv_r[zz])
                                cp = ((l * NFC + jf) * 2 + part) * 4

                                def mm(wsb=wsb, zb=zb):
                                    ins = None
                                    for kc in range(KC):
                                        ins = nc.tensor.matmul(bank(zb)[:, lo_c:hi_c], wsb[:, kc, :],
                                                               h2T[:, kc, t0 - 1 + lo_c:t0 - 1 + hi_c],
                                                               start=(kc == 0), stop=(kc == KC - 1))
                                    return ins
                                op(PE, mm, reads=[ring_r[sl][ci][part]] + [h2_r[x] for x in
                                                                            range(max(0, tiles[0] - 1), min(NT, tiles[1] + 2))],
                                   writes=[pres[zb]])
                                op(ACT, lambda zb=zb, tt=tt, cp=cp: nc.scalar.activation(
                                    out=tt, in_=bank(zb)[:, 1:257], func=AF.Identity,
                                    scale=convp[:, cp + 1:cp + 2], bias=convp[:, cp + 3:cp + 4]),
                                   reads=[pres[zb]], writes=[tt_r])
                                op(DVE, lambda zb=zb, tt=tt, cp=cp: nc.vector.scalar_tensor_tensor(
                                    tt[:, lo:256], bank(zb)[:, lo:256], convp[:, cp:cp + 1], tt[:, lo:256], ALU.mult, ALU.add),
                                   reads=[pres[zb]], writes=[tt_r])
                                op(DVE, lambda zb=zb, tt=tt, cp=cp: nc.vector.scalar_tensor_tensor(
                                    tt[:, 0:hi], bank(zb)[:, 2:2 + hi], convp[:, cp + 2:cp + 3], tt[:, 0:hi], ALU.mult, ALU.add),
                                   reads=[pres[zb]], writes=[tt_r])
                            op(ACT, lambda zz=zz: nc.scalar.activation(out=tg[zz], in_=tg[zz], func=AF.Silu),
                               reads=[tg_r[zz]], writes=[tg_r[zz]])
                            op(DVE, lambda zz=zz, ci=ci, ab=ab: nc.vector.tensor_tensor(aT[ab][:, ci, :], tg[zz], tv[zz], ALU.mult),
                               reads=[tg_r[zz], tv_r[zz], aT_r[ab]], writes=[aT_r[ab]])
                        for tl_i, ti in enumerate(tiles):
                            for n in range(2):
                                bk = 4 + (tl_i * 2 + n)

                                def mmd(tl_i=tl_i, n=n, bk=bk):
                                    ins = None
                                    for ci in range(len(chunks)):
                                        ins = nc.tensor.matmul(bank(bk), aT[ab][:, ci, tl_i * 128:(tl_i + 1) * 128],
                                                               dn[sl][ci][:, n * 512:(n + 1) * 512],
                                                               start=(ci == 0), stop=(ci == len(chunks) - 1))
                                    return ins
                                op(PE, mmd, reads=[aT_r[ab]] + [ring_r[sl][ci][2] for ci in range(len(chunks))], writes=[pres[bk]])
                                op(DVE, lambda ti=ti, n=n, bk=bk: nc.vector.tensor_tensor(
                                    X[:, ti, n * 512:(n + 1) * 512], bank(bk), X[:, ti, n * 512:(n + 1) * 512], ALU.add),
                                   reads=[pres[bk], xres[ti]], writes=[xres[ti]])
                if self.dbg and (kind, b, l) in self.dbg_points:
                    self.dump(f"xffn_{kind}{b}_{l}", X, xres, NT)
                self.barrier()
                self.off = base

            self.X = X
            self.xres = xres

            def final_norm(b):
                base = self.off
                nrm = alloc(D)
                nrm_r = Res()
                junk = alloc(D, BF16)
                junk_r = Res()
                ot = [alloc(D) for _ in range(2)]
                ot_r = [Res(), Res()]
                load_row(nrm, nrm_r, fing_d[0:1, :])
                norm_stats(16, junk, junk_r)
                for i in range(16):
                    k = i % 2
                    op(DVE, lambda i=i, k=k: nc.vector.scalar_tensor_tensor(ot[k], X[:, i, :], rstd[:, i:i + 1], nrm, ALU.mult, ALU.mult),
                       reads=[xres[i], rstd_r, nrm_r], writes=[ot_r[k]])
                    self.out_toks.append(dma(out_d[b, i * 128:(i + 1) * 128, :], ot[k], reads=[ot_r[k]]))
                self.barrier()
                self.off = base

            for i in range(2 * NBC):
                dma(X[:, i, :], ctx_d[i // 2, (i % 2) * 128:(i % 2 + 1) * 128, :], writes=[xres[i]])
            cseqs = [(2 * s, 2) for s in range(NBC)]
            enqueue_layer(0, False)
            enqueue_layer(1, True)
            for bb in range(nb):
                enqueue_layer(0, False)
                enqueue_layer(1, False)
            self.issue_loads()
            run_layer("ctx", 4, 0, 2 * NBC, cseqs, CL, False, 0)
            run_layer("ctx", 4, 1, 2 * NBC, cseqs, CL, True, 0)
            for b in range(nb):
                for i in range(16):
                    dma(X[:, i, :], x_d[b, i * 128:(i + 1) * 128, :], writes=[xres[i]])
                for l in range(2):
                    run_layer("lat", b, l, 16, [(0, 16)], S, False, b)
                final_norm(b)
            for t in self.out_toks:
                self.SP.wait(t)
            for t in self.dbg_toks:
                self.SP.wait(t)
        return nc

    def dump(self, name, X, xres, NT):
        t = self.nc.dram_tensor(name, [NT * 128, D], F32, kind="ExternalOutput").ap()
        self.dbg_outs[name] = t
        for i in range(NT):
            self.dbg_toks.append(self.dma(t[i * 128:(i + 1) * 128, :], X[:, i, :], reads=[xres[i]]))


def _host_prep(inp, core, nb=NBC):
    f = np.float32
    b0 = core * NBC
    x = np.ascontiguousarray(inp["x"][b0:b0 + nb], dtype=f)
    ctx = np.ascontiguousarray(inp["ctx"][b0:b0 + NBC], dtype=f)
    rows = np.concatenate([inp["c"][b0:b0 + NBC], inp["c_ctx"][None, :]], axis=0).astype(f)
    cT = np.ascontiguousarray(rows.reshape(5, KC, 128).transpose(2, 1, 0).reshape(128, 40))
    m = {"x": x, "ctx": ctx, "cT": cT}
    return m


def _shared_prep(inp):
    f = np.float32
    order = [0, 4, 1, 5, 2, 6, 3, 7]
    perm_q = np.concatenate([h * 64 + np.arange(64) for h in order])
    w_in = np.asarray(inp["w_in"], dtype=f)
    w_in_tm = np.ascontiguousarray(np.concatenate(
        [w_in[:, :, perm_q], w_in[:, :, 512:640], w_in[:, :, 640:768], w_in[:, :, 1024:1280]], axis=2))
    w_in_fm = np.ascontiguousarray(np.concatenate([w_in[:, :, 768:1024], w_in[:, :, 1280:1536]], axis=2))
    row_perm = np.concatenate([perm_q, 512 + np.arange(512)])
    w_out_p = np.ascontiguousarray(np.asarray(inp["w_out"], dtype=f)[:, row_perm, :])
    w_up = np.asarray(inp["w_up"], dtype=f)
    wu = w_up.reshape(2, KC, 128, 2, NFC, 128)
    w_up_r = np.ascontiguousarray(wu.transpose(0, 4, 3, 2, 1, 5).reshape(2, 2 * NFC, 128, 1024))
    gains = np.ascontiguousarray(np.concatenate([inp["q_gain"], inp["k_gain"]], axis=1).astype(f).reshape(1, 256))
    w_s = np.asarray(inp["w_s"], dtype=f)
    wsT = np.ascontiguousarray(w_s.transpose(0, 3, 1, 2).reshape(2, 128, 512))
    bs = np.ascontiguousarray(np.asarray(inp["b_s"], dtype=f).reshape(1, 1024))
    w_pool = np.asarray(inp["w_pool"], dtype=f)
    wbd = np.zeros((2, 128, 256), f)
    for l in range(2):
        for c in range(2):
            wbd[l, 0:64, c * 128:c * 128 + 64] = w_pool[l, 2 * c]
            wbd[l, 64:128, c * 128 + 64:c * 128 + 128] = w_pool[l, 2 * c + 1]
    ps = np.asarray(inp["pool_scale"], dtype=f)
    pscale = np.ascontiguousarray(ps.reshape(2, 2, 128).transpose(2, 0, 1).reshape(128, 4))
    conv_w = np.asarray(inp["conv_w"], dtype=f)
    conv_b = np.asarray(inp["conv_b"], dtype=f)
    cw = conv_w.reshape(2, 3, 2, NFC, 128)
    cb = conv_b.reshape(2, 1, 2, NFC, 128)
    cc = np.concatenate([cw, cb], axis=1)
    convp = np.ascontiguousarray(cc.transpose(4, 0, 3, 2, 1).reshape(128, 2 * NFC * 2 * 4))
    t = np.arange(S)
    rowi = (t // 64).astype(f)
    coli = (t % 64).astype(f)
    inv = (np.float32(10000.0) ** (-np.arange(0, 32, 2, dtype=f) / np.float32(32))).astype(f)
    ang = np.stack([rowi[:, None] * inv[None, :], coli[:, None] * inv[None, :]], axis=1).astype(f)
    cos = np.cos(ang).astype(f)
    sin = np.sin(ang).astype(f)
    C = np.repeat(cos[:, :, None, :], 2, axis=2).reshape(S, 64)
    ropeC = np.ascontiguousarray(C.reshape(16, 128, 64).transpose(1, 0, 2).reshape(128, 16 * 64))
    ropeS = np.ascontiguousarray(sin.reshape(16, 128, 32).transpose(1, 0, 2).reshape(128, 16 * 32))

    def invcnt(L, nseq):
        W = nseq * (L + 32)
        o = np.zeros((128, 2, W), f)
        tt = np.arange(L)
        for c in range(2):
            for r in range(2):
                w = POOL_W[2 * c + r]
                left = w // 2
                right = w - 1 - left
                cnt = (np.minimum(tt + right, L - 1) - np.maximum(tt - left, 0) + 1).astype(f)
                for s in range(nseq):
                    o[r * 64:(r + 1) * 64, c, s * (L + 32) + 16: s * (L + 32) + 16 + L] = (np.float32(1.0) / cnt)[None, :]
        return np.ascontiguousarray(o.reshape(128, 2 * W))
    sh = {
        "w_mod": np.ascontiguousarray(inp["w_mod"], dtype=f), "b_mod": np.ascontiguousarray(inp["b_mod"], dtype=f),
        "w_in_tm": w_in_tm, "w_in_fm": w_in_fm, "w_out_p": w_out_p, "w_up_r": w_up_r,
        "w_down": np.ascontiguousarray(inp["w_down"], dtype=f), "gains": gains, "wsT": wsT, "bs": bs, "wbd": wbd,
        "pscale": pscale, "convp": convp,
        "n1g": np.ascontiguousarray(inp["norm1_g"], dtype=f), "n2g": np.ascontiguousarray(inp["norm2_g"], dtype=f),
        "fing": np.ascontiguousarray(np.asarray(inp["final_g"], dtype=f).reshape(1, D)),
        "ropeC": ropeC, "ropeS": ropeS, "invc_lat": invcnt(S, 1), "invc_ctx": invcnt(CL, NBC),
    }
    return sh


NB_PER_LAUNCH = 4


def kernel(**inputs):
    inp = {k: np.asarray(v) for k, v in inputs.items()}
    sh = _shared_prep(inp)
    outs = [[] for _ in range(N_CORES)]
    for b_lo in range(0, NBC, NB_PER_LAUNCH):
        bld = Builder(nb=NB_PER_LAUNCH)
        nc = bld.build()
        in_maps = []
        for c in range(N_CORES):
            m = dict(sh)
            hp = _host_prep(inp, c)
            hp["x"] = np.ascontiguousarray(hp["x"][b_lo:b_lo + NB_PER_LAUNCH])
            if b_lo:
                hp["ctx"] = np.ascontiguousarray(np.roll(hp["ctx"], -b_lo, axis=0))
                cT = hp["cT"].reshape(128, KC, 5)
                cT = np.concatenate([np.roll(cT[:, :, 0:4], -b_lo, axis=2), cT[:, :, 4:5]], axis=2)
                hp["cT"] = np.ascontiguousarray(cT.reshape(128, 40))
            m.update(hp)
            in_maps.append(m)
        res = run_bass_kernel_spmd(nc, in_maps, core_ids=list(range(N_CORES)))
        for c in range(N_CORES):
            outs[c].append(np.asarray(res.results[c]["out"]))
    out = np.concatenate([np.concatenate(o, axis=0) for o in outs], axis=0)
    return out.astype(np.float32)
```
